# Optimizing a Trainium2 kernel written in Bass

```python
import jax, jax.numpy as jnp
from jax import lax
import numpy as np

D_MODEL = 1024
BATCH = 4
SEQ = 4096
DEPTH = 2

MEM_LEN = 256
GRID_W = 64
Q_BLOCK = 128
EPS = 1e-6

ATTN_HEADS = 8
ATTN_KV_HEADS = 2
HEAD_DIM = 64
ATTN_Q_W = ATTN_HEADS * HEAD_DIM
ATTN_KV_W = ATTN_KV_HEADS * HEAD_DIM
ROPE_THETA = 10000.0

GMLP_W = 512
GMLP_GROUPS = 4
GMLP_GROUP_W = GMLP_W // GMLP_GROUPS
GMLP_CHUNK = 128

LRU_W = 512
LRU_HEADS = 8
LRU_HEAD_W = LRU_W // LRU_HEADS
CONV_W = 4
LRU_C = 8.0
N_DIR = 2

N_BRANCH = 3

XATTN_HEADS = 4
XATTN_HEAD_DIM = D_MODEL // XATTN_HEADS

D_FF = 2816

MIX_IN_W = ATTN_Q_W + 2 * ATTN_KV_W + 2 * GMLP_W + 2 * LRU_W + N_BRANCH * D_MODEL
MIX_IN_SPLITS = (
    ATTN_Q_W,
    ATTN_Q_W + ATTN_KV_W,
    ATTN_Q_W + 2 * ATTN_KV_W,
    ATTN_Q_W + 2 * ATTN_KV_W + GMLP_W,
    ATTN_Q_W + 2 * ATTN_KV_W + 2 * GMLP_W,
    ATTN_Q_W + 2 * ATTN_KV_W + 2 * GMLP_W + LRU_W,
    ATTN_Q_W + 2 * ATTN_KV_W + 2 * GMLP_W + 2 * LRU_W,
)

kernel_name = "hybrid_gated_parallel_encoder"


def rms_norm(x, g):
    xf = x.astype(jnp.float32)
    y = xf * lax.rsqrt(jnp.mean(xf * xf, axis=-1, keepdims=True) + EPS)
    return (y * g.astype(jnp.float32)).astype(x.dtype)


def swiglu(h, w_in, w_out):
    a, b = jnp.split(h @ w_in, 2, axis=-1)
    return (jax.nn.silu(a) * b) @ w_out


def axial_rope_tables(seq_len):
    rows = seq_len // GRID_W
    row = jnp.repeat(jnp.arange(rows), GRID_W).astype(jnp.float32)
    col = jnp.tile(jnp.arange(GRID_W), rows).astype(jnp.float32)
    n_freq = HEAD_DIM // 4
    inv_freq = ROPE_THETA ** (-jnp.arange(n_freq, dtype=jnp.float32) / n_freq)
    ang_r = row[:, None] * inv_freq[None, :]
    ang_c = col[:, None] * inv_freq[None, :]
    return (jnp.cos(ang_r), jnp.sin(ang_r), jnp.cos(ang_c), jnp.sin(ang_c))


def _rotate(x, cos, sin):
    x1, x2 = jnp.split(x, 2, axis=-1)
    c = cos[:, None, :]
    s = sin[:, None, :]
    return jnp.concatenate([x1 * c - x2 * s, x2 * c + x1 * s], axis=-1)


def apply_axial_rope(x, tabs):
    cos_r, sin_r, cos_c, sin_c = tabs
    xf = x.astype(jnp.float32)
    x_row, x_col = jnp.split(xf, 2, axis=-1)
    out = jnp.concatenate([_rotate(x_row, cos_r, sin_r), _rotate(x_col, cos_c, sin_c)], axis=-1)
    return out.astype(x.dtype)


def gqa_block_attention(q, k, v):
    b, s, _, _ = q.shape
    nb = s // Q_BLOCK
    grp = ATTN_HEADS // ATTN_KV_HEADS
    qb = q.reshape(b, nb, Q_BLOCK, ATTN_KV_HEADS, grp, HEAD_DIM).transpose(1, 0, 2, 3, 4, 5)
    scale = HEAD_DIM ** -0.5

    def one_block(qi):
        sc = jnp.einsum('bqkgd,bskd->bkgqs', qi, k).astype(jnp.float32) * scale
        p = jax.nn.softmax(sc, axis=-1).astype(v.dtype)
        return jnp.einsum('bkgqs,bskd->bqkgd', p, v)

    o = lax.map(one_block, qb)
    return o.transpose(1, 0, 2, 3, 4, 5).reshape(b, s, ATTN_Q_W)


def gmlp_branch(u, v, v_norm, ws, bs):
    b, s, _ = u.shape
    nc = s // GMLP_CHUNK
    u = jax.nn.gelu(u)
    v = rms_norm(jax.nn.gelu(v), v_norm)
    vc = v.reshape(b, nc, GMLP_CHUNK, GMLP_GROUPS, GMLP_GROUP_W)
    sv = jnp.einsum('gpq,bcqgd->bcpgd', ws, vc) + bs.T[:, :, None]
    return u * sv.reshape(b, s, GMLP_W)


def depthwise_conv_centred(x, w, bias):
    out = lax.conv_general_dilated(
        x, w[:, None, :], window_strides=(1,),
        padding=[(CONV_W // 2, CONV_W - 1 - CONV_W // 2)],
        dimension_numbers=('NWC', 'WIO', 'NWC'),
        feature_group_count=x.shape[-1])
    return out + bias


def block_diag(x, w, bias):
    b, s, _ = x.shape
    y = jnp.einsum('bshi,hio->bsho', x.reshape(b, s, LRU_HEADS, LRU_HEAD_W), w)
    return y.reshape(b, s, LRU_W) + bias


def rg_lru(x, wa, ba, wi, bi, lam, reverse):
    r = jax.nn.sigmoid(block_diag(x, wa, ba).astype(jnp.float32))
    i = jax.nn.sigmoid(block_diag(x, wi, bi).astype(jnp.float32))
    log_a = -LRU_C * r * jax.nn.softplus(-lam.astype(jnp.float32))
    a = jnp.exp(log_a)
    bx = jnp.sqrt(-jnp.expm1(2.0 * log_a)) * (i * x.astype(jnp.float32))

    def combine(e1, e2):
        a1, b1 = e1
        a2, b2 = e2
        return a1 * a2, a2 * b1 + b2

    _, h = lax.associative_scan(combine, (a, bx), axis=1, reverse=reverse)
    return h.astype(x.dtype)


def lru_branch(xl, yl, conv_w, conv_b, wa, ba, wi, bi, lam):
    xc = depthwise_conv_centred(xl, conv_w, conv_b)
    h = (rg_lru(xc, wa[0], ba[0], wi[0], bi[0], lam[0], False)
         + rg_lru(xc, wa[1], ba[1], wi[1], bi[1], lam[1], True))
    return h * jax.nn.gelu(yl)


def cross_attention(h, mem_n, wq, wkv, wo):
    b, s, _ = h.shape
    m = mem_n.shape[1]
    q = (h @ wq).reshape(b, s, XATTN_HEADS, XATTN_HEAD_DIM)
    k, v = jnp.split(mem_n @ wkv, 2, axis=-1)
    k = k.reshape(b, m, XATTN_HEADS, XATTN_HEAD_DIM)
    v = v.reshape(b, m, XATTN_HEADS, XATTN_HEAD_DIM)
    sc = jnp.einsum('bqhd,bmhd->bhqm', q, k).astype(jnp.float32) * (XATTN_HEAD_DIM ** -0.5)
    p = jax.nn.softmax(sc, axis=-1).astype(v.dtype)
    o = jnp.einsum('bhqm,bmhd->bqhd', p, v).reshape(b, s, D_MODEL)
    return o @ wo


def setup_inputs(seed: int = 0) -> dict:
    key = jax.random.key(seed)
    ks = iter(jax.random.split(key, 40))
    L = DEPTH

    def nrm(shape, scale):
        return jax.random.normal(next(ks), shape, jnp.float32) * scale

    def gain(shape):
        return 1.0 + nrm(shape, 0.01)

    x = nrm((BATCH, SEQ, D_MODEL), 1.0)
    mem = nrm((BATCH, MEM_LEN, D_MODEL), 1.0)
    ffn1_norm = gain((L, D_MODEL))
    ffn1_w_in = nrm((L, D_MODEL, 2 * D_FF), D_MODEL ** -0.5)
    ffn1_w_out = nrm((L, D_FF, D_MODEL), D_FF ** -0.5)
    mix_norm = gain((L, D_MODEL))
    w_mix_in = nrm((L, D_MODEL, MIX_IN_W), D_MODEL ** -0.5)
    b_gate = nrm((L, N_BRANCH * D_MODEL), 0.01)
    q_norm = gain((L, HEAD_DIM))
    k_norm = gain((L, HEAD_DIM))
    attn_up = nrm((L, ATTN_Q_W, D_MODEL), ATTN_Q_W ** -0.5)
    gmlp_v_norm = gain((L, GMLP_W))
    gmlp_ws = nrm((L, GMLP_GROUPS, GMLP_CHUNK, GMLP_CHUNK), 0.5 * GMLP_CHUNK ** -0.5)
    gmlp_bs = gain((L, GMLP_GROUPS, GMLP_CHUNK))
    gmlp_up = nrm((L, GMLP_W, D_MODEL), GMLP_W ** -0.5)
    lru_conv_w = nrm((L, CONV_W, LRU_W), CONV_W ** -0.5)
    lru_conv_b = nrm((L, LRU_W), 0.01)
    lru_wa = nrm((L, N_DIR, LRU_HEADS, LRU_HEAD_W, LRU_HEAD_W), LRU_HEAD_W ** -0.5)
    lru_ba = nrm((L, N_DIR, LRU_W), 0.01)
    lru_wi = nrm((L, N_DIR, LRU_HEADS, LRU_HEAD_W, LRU_HEAD_W), LRU_HEAD_W ** -0.5)
    lru_bi = nrm((L, N_DIR, LRU_W), 0.01)
    a_pow_c = jax.random.uniform(next(ks), (L, N_DIR, LRU_W), jnp.float32, 0.9, 0.999)
    a0 = a_pow_c ** (1.0 / LRU_C)
    lru_lambda = jnp.log(a0) - jnp.log1p(-a0)
    lru_up = nrm((L, LRU_W, D_MODEL), LRU_W ** -0.5)
    w_mix_out = nrm((L, D_MODEL, D_MODEL), D_MODEL ** -0.5)
    xattn_norm = gain((L, D_MODEL))
    mem_norm = gain((L, D_MODEL))
    xattn_wq = nrm((L, D_MODEL, D_MODEL), D_MODEL ** -0.5)
    xattn_wkv = nrm((L, D_MODEL, 2 * D_MODEL), D_MODEL ** -0.5)
    xattn_wo = nrm((L, D_MODEL, D_MODEL), D_MODEL ** -0.5)
    ffn2_norm = gain((L, D_MODEL))
    ffn2_w_in = nrm((L, D_MODEL, 2 * D_FF), D_MODEL ** -0.5)
    ffn2_w_out = nrm((L, D_FF, D_MODEL), D_FF ** -0.5)
    final_norm = gain((D_MODEL,))
    return {
        "x": x, "mem": mem,
        "ffn1_norm": ffn1_norm, "ffn1_w_in": ffn1_w_in, "ffn1_w_out": ffn1_w_out,
        "mix_norm": mix_norm, "w_mix_in": w_mix_in, "b_gate": b_gate,
        "q_norm": q_norm, "k_norm": k_norm, "attn_up": attn_up,
        "gmlp_v_norm": gmlp_v_norm, "gmlp_ws": gmlp_ws, "gmlp_bs": gmlp_bs, "gmlp_up": gmlp_up,
        "lru_conv_w": lru_conv_w, "lru_conv_b": lru_conv_b,
        "lru_wa": lru_wa, "lru_ba": lru_ba, "lru_wi": lru_wi, "lru_bi": lru_bi,
        "lru_lambda": lru_lambda, "lru_up": lru_up,
        "w_mix_out": w_mix_out,
        "xattn_norm": xattn_norm, "mem_norm": mem_norm,
        "xattn_wq": xattn_wq, "xattn_wkv": xattn_wkv, "xattn_wo": xattn_wo,
        "ffn2_norm": ffn2_norm, "ffn2_w_in": ffn2_w_in, "ffn2_w_out": ffn2_w_out,
        "final_norm": final_norm,
    }


def reference(x, mem, ffn1_norm, ffn1_w_in, ffn1_w_out, mix_norm, w_mix_in, b_gate,
              q_norm, k_norm, attn_up, gmlp_v_norm, gmlp_ws, gmlp_bs, gmlp_up,
              lru_conv_w, lru_conv_b, lru_wa, lru_ba, lru_wi, lru_bi, lru_lambda, lru_up,
              w_mix_out, xattn_norm, mem_norm, xattn_wq, xattn_wkv, xattn_wo,
              ffn2_norm, ffn2_w_in, ffn2_w_out, final_norm):
    b, s, _ = x.shape
    tabs = axial_rope_tables(s)
    for l in range(DEPTH):
        x = x + 0.5 * swiglu(rms_norm(x, ffn1_norm[l]), ffn1_w_in[l], ffn1_w_out[l])

        h = rms_norm(x, mix_norm[l])
        q, k, v, gu, gv, lx, ly, g = jnp.split(h @ w_mix_in[l], MIX_IN_SPLITS, axis=-1)

        q = apply_axial_rope(rms_norm(q.reshape(b, s, ATTN_HEADS, HEAD_DIM), q_norm[l]), tabs)
        k = apply_axial_rope(rms_norm(k.reshape(b, s, ATTN_KV_HEADS, HEAD_DIM), k_norm[l]), tabs)
        v = v.reshape(b, s, ATTN_KV_HEADS, HEAD_DIM)
        y_attn = gqa_block_attention(q, k, v) @ attn_up[l]

        y_gmlp = gmlp_branch(gu, gv, gmlp_v_norm[l], gmlp_ws[l], gmlp_bs[l]) @ gmlp_up[l]

        y_lru = lru_branch(lx, ly, lru_conv_w[l], lru_conv_b[l], lru_wa[l], lru_ba[l],
                           lru_wi[l], lru_bi[l], lru_lambda[l]) @ lru_up[l]

        gates = jax.nn.sigmoid((g + b_gate[l]).astype(jnp.float32)).astype(x.dtype)
        gates = gates.reshape(b, s, N_BRANCH, D_MODEL)
        merged = gates[:, :, 0] * y_attn + gates[:, :, 1] * y_gmlp + gates[:, :, 2] * y_lru
        x = x + merged @ w_mix_out[l]

        x = x + cross_attention(rms_norm(x, xattn_norm[l]), rms_norm(mem, mem_norm[l]),
                                xattn_wq[l], xattn_wkv[l], xattn_wo[l])

        x = x + 0.5 * swiglu(rms_norm(x, ffn2_norm[l]), ffn2_w_in[l], ffn2_w_out[l])
    return rms_norm(x, final_norm)
```

```python
import contextlib
import os
import numpy as np
import ml_dtypes
import concourse.bass as bass
import concourse.mybir as mybir
from concourse.bass_utils import run_bass_kernel_spmd

F32 = mybir.dt.float32
BF16 = mybir.dt.bfloat16
AF = mybir.ActivationFunctionType
ALU = mybir.AluOpType

T = 2048
NT = 16
D = 1024
DFF = 2816
NJ = 22
EPS = 1e-6
NPV = 1168


class Prog:
    ENGS = ("pe", "act", "dve", "pool", "sp")
    SAME_ENGINE_SYNC = ("act", "dve", "pool")

    def __init__(self, nc):
        self.nc = nc
        self.ops = []
        self.last_w = {}
        self.readers = {}

    def add(self, eng, fn, r=(), w=(), dma=None):
        idx = len(self.ops)
        strong, war = set(), set()
        for t in r:
            if t in self.last_w:
                strong.add(self.last_w[t])
            if isinstance(t, tuple) and t[0] == "ps":
                for i in self.readers.get(t, ()):
                    war.add(i)
        for t in w:
            if t in self.last_w:
                strong.add(self.last_w[t])
            for i in self.readers.get(t, ()):
                war.add(i)
        for t in w:
            self.last_w[t] = idx
            self.readers[t] = []
        for t in r:
            self.readers.setdefault(t, []).append(idx)
        strong.discard(idx)
        war.discard(idx)
        self.ops.append(dict(eng=eng, fn=fn, strong=strong, war=war - strong, dma=dma))
        return idx

    def barrier(self):
        tails = {}
        for i, op in enumerate(self.ops):
            if op["fn"] is None:
                continue
            if op["dma"] is not None:
                tails[("dma", op["dma"])] = i
            else:
                tails[("eng", op["eng"])] = i
        tl = set(tails.values())
        for e in self.ENGS:
            i = self.add(e, None)
            self.ops[i]["strong"] = set(tl)
        self.last_w = {}
        self.readers = {}

    def barrier_keep(self, toks):
        keep = {t: self.last_w[t] for t in toks if t in self.last_w}
        self.barrier()
        self.last_w.update(keep)

    def emit(self):
        nc = self.nc
        ops = self.ops
        n = len(ops)
        need = [[] for _ in range(n)]
        signal = [False] * n
        for j, op in enumerate(ops):
            F = op["eng"]
            for i in sorted(op["strong"] | op["war"]):
                p = ops[i]
                if p["fn"] is None:
                    continue
                if p["dma"] is None and p["eng"] == F:
                    if F not in self.SAME_ENGINE_SYNC:
                        continue
                need[j].append(i)
                signal[i] = True
        cnt = {}
        val = [None] * n
        for i, op in enumerate(ops):
            if op["fn"] is None:
                continue
            if op["dma"] is not None:
                k = ("dma", op["dma"])
                cnt[k] = cnt.get(k, 0) + 16
                val[i] = (k, cnt[k])
            elif signal[i]:
                k = ("eng", op["eng"])
                cnt[k] = cnt.get(k, 0) + 1
                val[i] = (k, cnt[k])
        keys = sorted(set(v[0] for v in val if v is not None), key=str)
        with contextlib.ExitStack() as st:
            sems = {}
            for n_, k in enumerate(keys):
                sems[k] = st.enter_context(nc.semaphore("s%d" % n_))
            block = st.enter_context(nc.Block())

            def run(engname, eh):
                known = {}
                for j, op in enumerate(ops):
                    if op["eng"] != engname:
                        continue
                    w = {}
                    for i in need[j]:
                        k, c = val[i]
                        if c > w.get(k, 0):
                            w[k] = c
                    for k, c in w.items():
                        if known.get(k, 0) >= c:
                            continue
                        eh.wait_ge(sems[k], c)
                        known[k] = c
                    if op["fn"] is None:
                        continue
                    ins = op["fn"](eh)
                    if val[j] is not None:
                        k, c = val[j]
                        ins.then_inc(sems[k], 16 if k[0] == "dma" else 1)

            @block.tensor
            def _(e):
                run("pe", e)

            @block.scalar
            def _(e):
                run("act", e)

            @block.vector
            def _(e):
                run("dve", e)

            @block.gpsimd
            def _(e):
                run("pool", e)

            @block.sync
            def _(e):
                run("sp", e)


def build(seg):
    nc = bass.Bass("TRN2", target_bir_lowering=False)
    P = Prog(nc)
    ins = {}
    outs = {}

    def DIN(name, shape, dt=F32):
        if name not in ins:
            ins[name] = nc.dram_tensor(name, list(shape), dt, kind="ExternalInput").ap()
        return ins[name]

    def DOUT(name, shape, dt=F32):
        outs[name] = nc.dram_tensor(name, list(shape), dt, kind="ExternalOutput").ap()
        return outs[name]

    cnt = [0]

    def SB(off, shape, dt):
        cnt[0] += 1
        return nc.alloc_sbuf_tensor_at("t%d" % cnt[0], [128] + list(shape), dt, offset=off + 16640)

    psum = nc.alloc_psum_tensor("psall", [128, 4096], F32)

    FUSED = (seg == "fused")
    I32 = mybir.dt.int32
    if FUSED:
        sh_kt = [nc.dram_tensor("sh_kt%d" % l_, [2, 128, T], F32, addr_space="Shared").ap() for l_ in range(2)]
        sh_v = [nc.dram_tensor("sh_v%d" % l_, [2, 128, NT, 192], F32, addr_space="Shared").ap() for l_ in range(2)]
        sh_tl = [nc.dram_tensor("sh_tl%d" % l_, [2, 128, 16], F32, addr_space="Shared").ap() for l_ in range(2)]
        sh_hs = [nc.dram_tensor("sh_hs%d" % l_, [2, 128, 4], F32, addr_space="Shared").ap() for l_ in range(2)]
        sh_flag = [nc.dram_tensor("sh_flag%d" % l_, [2, 16], I32, addr_space="Shared").ap() for l_ in range(2)]
    MAGIC = [41017, 52309]

    def DMA_DYN(q, mk, r, w, key):
        def fn(e):
            par = get_par(e, q)
            o, i_ = mk(e, par)
            return e.dma_start(out=o, in_=i_)
        P.add(q, fn, r=r, w=w, dma=key)

    par_cache = {}

    def get_par(e, q):
        if q not in par_cache:
            par_cache[q] = e.snap(e.partition_id() % 2, min_val=0, max_val=1)
        return par_cache[q]

    class Bump:
        def __init__(self, start, end=None):
            self.p = start
            self.end = end
        def t(self, shape, dt):
            nb = int(np.prod(shape)) * (4 if dt == F32 else 2)
            nb = (nb + 63) // 64 * 64
            o = self.p
            self.p += nb
            assert self.p <= (self.end or GEND), (self.p, self.end, GEND)
            return SB(o, shape, dt)

    def PS(b, n=512, nb=1):
        return psum[:, b * 512:b * 512 + n]

    def pst(b):
        return ("ps", b)

    def PE(mms, r, w):
        def fn(e, mms=mms):
            i_ = None
            for (o, l, rr, s0, s1) in mms:
                i_ = e.matmul(o, lhsT=l, rhs=rr, start=s0, stop=s1)
            return i_
        P.add("pe", fn, r=r, w=w)

    def ACT(out, in_, func, r, w, **kw):
        P.add("act", lambda e: e.activation(out=out, in_=in_, func=func, **kw), r=r, w=w)

    def DVE(meth, r, w, **kw):
        P.add("dve", lambda e: getattr(e, meth)(**kw), r=r, w=w)

    def DMA(q, out, in_, r, w, key):
        P.add(q, lambda e: e.dma_start(out=out, in_=in_), r=r, w=w, dma=key)

    def TS(out, in0, s1, s2, op0, op1, r, w):
        DVE("tensor_scalar", r, w, out=out, in0=in0, scalar1=s1, scalar2=s2, op0=op0, op1=op1)

    def TT(out, in0, in1, op, r, w):
        DVE("tensor_tensor", r, w, out=out, in0=in0, in1=in1, op=op)

    def STT(out, in0, scalar, in1, op0, op1, r, w):
        DVE("scalar_tensor_tensor", r, w, out=out, in0=in0, scalar=scalar, in1=in1, op0=op0, op1=op1)

    x_sb = SB(0, [NT, D], F32)
    hT = SB(65536, [8, T], BF16)
    cm = SB(98304, [4, 128], BF16)
    pv = SB(99328, [NPV], F32)
    pvx = SB(104000, [32], F32)
    sm = SB(104128, [64], F32)
    G0 = 104448
    GEND = 212480
    YL_OFF = GEND - 16384
    AO_OFF = GEND - 32768
    YG_OFF = GEND - 49152

    ident = cm[:, 0, :]
    Jm = cm[:, 1, :]
    bones = cm[:, 2, :]
    pswap = cm[:, 3, :]

    DMA("pool", cm[:], DIN("cm", [128, 4, 128]), [], ["cm"], "cm")

    def load_pv(l):
        DMA("sp", pv[:], DIN("pv%d" % l, [128, NPV]), [], ["pv"], "pv")
        ACT(pvx[:, 0:8], pv[:, 94:102], AF.Exp, ["pv"], ["pvx"], scale=-1.0)
        ACT(pvx[:, 0:8], pvx[:, 0:8], AF.Ln, ["pvx"], ["pvx"], bias=1.0, scale=1.0)
        TS(pvx[:, 8:16], pvx[:, 0:8], 8.0, None, ALU.mult, ALU.bypass, ["pvx"], ["pvx2"])
        TS(pvx[:, 0:8], pvx[:, 0:8], -8.0, None, ALU.mult, ALU.bypass, ["pvx", "pvx2"], ["pvx"])

    def gelu(out, in_, n, scr0, scr1, r, w, tag):
        ACT(scr0, in_, AF.Square, r, [tag + "s0"])
        TS(scr0, scr0, 0.044715, 1.0, ALU.mult, ALU.add, [tag + "s0"], [tag + "s0"])
        TT(scr1, scr0, in_, ALU.mult, r + [tag + "s0"], [tag + "s1"])
        ACT(scr1, scr1, AF.Sigmoid, [tag + "s1"], [tag + "s1"], scale=1.5957691216057308)
        TT(out, scr1, in_, ALU.mult, r + [tag + "s1"], w)

    def norm_to_hT(gcol, src=None, ntiles=NT, dst=None, base=G0, rtok="x", wtok="hT"):
        src = x_sb if src is None else src
        dst = hT if dst is None else dst
        junk = SB(base, [D], BF16)
        xn = [SB(base + 2048, [D], BF16), SB(base + 4096, [D], BF16)]
        for t in range(ntiles):
            ss = sm[:, (t % 2) * 4:(t % 2) * 4 + 1]
            rs = sm[:, (t % 2) * 4 + 1:(t % 2) * 4 + 2]
            stk = ("nst", t % 2)
            P.add("act", lambda e, t=t, ss=ss: e.activation(out=junk[:], in_=src[:, t, :], func=AF.Square, accum_out=ss),
                  r=[rtok], w=["junk", stk])
            ACT(rs, ss, AF.Sqrt, [stk], [stk], scale=1.0 / D, bias=EPS)
            DVE("reciprocal", [stk], [stk], out=rs, in_=rs)
            xt = xn[t % 2]
            TS(xt[:], src[:, t, :], rs, None, ALU.mult, ALU.bypass, [rtok, stk], [("xn", t % 2)])
            for half in range(2):
                b = (2 * t + half) % 8
                PE([(PS(b)[:, q * 128:(q + 1) * 128], xt[:, (half * 4 + q) * 128:(half * 4 + q + 1) * 128], ident, True, True)
                    for q in range(4)], [("xn", t % 2), "cm"], [pst(b)])
                o = dst[:, half * 4:half * 4 + 4, t * 128:(t + 1) * 128]
                i0 = PS(b).rearrange("p (q n) -> p q n", q=4)
                g = pv[:, gcol + half * 4:gcol + half * 4 + 4].unsqueeze(2).to_broadcast([128, 4, 128])
                TT(o, i0, g, ALU.mult, [pst(b), "pv"], [wtok])

    wslot_ctr = {}

    def wslots(name, base, shape, n):
        tiles = [SB(base + i * int(np.prod(shape)) * 2, shape, BF16) for i in range(n)]
        wslot_ctr.setdefault(name, 0)

        def nxt():
            i = wslot_ctr[name] % n
            wslot_ctr[name] += 1
            return tiles[i], (name, i)
        return nxt

    def LW(dst, src, tok):
        DMA("pool", dst, src, [], [tok], "w_%s_%d" % tok)

    def ffn(l, which):
        win = DIN("f%d_win%d" % (which, l), [128, NJ, 8, 256])
        wout = DIN("f%d_wout%d" % (which, l), [128, NJ, D])
        GT = SB(G0, [NJ, 1024], BF16)
        b1 = G0 + 45056
        nwi = wslots("wi", b1, [8, 256], 3)
        b2 = b1 + 3 * 4096
        WO = SB(b2, [NJ, D], BF16)
        b3 = b2 + 45056
        assert b3 + 4096 <= GEND
        sil = [SB(b3, [512], F32), SB(b3 + 2048, [512], F32)]
        for blk in range(2):
            t0 = blk * 1024
            k = 0
            for j in range(NJ):
                wt, wtok = nwi()
                LW(wt[:], win[:, j, :, :], wtok)
                if blk == 0 and j == 2:
                    LW(WO[:, 0:11, :], wout[:, 0:11, :], ("woA", 0))
                if blk == 0 and j == 8:
                    LW(WO[:, 11:NJ, :], wout[:, 11:NJ, :], ("woB", 0))
                for sb_ in range(2):
                    tok = slice(t0 + sb_ * 512, t0 + sb_ * 512 + 512)
                    ba, bb = (k % 4) * 2, (k % 4) * 2 + 1
                    PE([(PS(ba), wt[:, c, 0:128], hT[:, c, tok], c == 0, c == 7) for c in range(8)], [wtok, "hT"], [pst(ba)])
                    PE([(PS(bb), wt[:, c, 128:256], hT[:, c, tok], c == 0, c == 7) for c in range(8)], [wtok, "hT"], [pst(bb)])
                    s = sil[k % 2]
                    ACT(s[:], PS(ba), AF.Silu, [pst(ba)], [("sil", k % 2)])
                    TT(GT[:, j, sb_ * 512:sb_ * 512 + 512], s[:], PS(bb), ALU.mult, [("sil", k % 2), pst(bb)], [("GT", j)])
                    k += 1
            for rnd in range(2):
                tiles = [blk * 8 + rnd * 4 + q for q in range(4)]
                for j in range(NJ):
                    wtok = ("woA", 0) if j < 11 else ("woB", 0)
                    mm = []
                    for q in range(4):
                        tl = slice((rnd * 4 + q) * 128, (rnd * 4 + q + 1) * 128)
                        for h in range(2):
                            mm.append((PS(2 * q + h), GT[:, j, tl], WO[:, j, h * 512:(h + 1) * 512], j == 0, j == NJ - 1))
                    PE(mm, [wtok, ("GT", j)], [pst(b) for b in range(8)])
                for q in range(4):
                    for h in range(2):
                        xs = x_sb[:, tiles[q], h * 512:(h + 1) * 512]
                        STT(xs, PS(2 * q + h), 0.5, xs, ALU.mult, ALU.add, [pst(2 * q + h), "x"], ["x"])
        P.barrier()

    def qk_post(b_in, gcol, tb, base, out, wtok, ctab, stab):
        sq = SB(base, [512], BF16)
        kg = SB(base + 1024, [512], BF16)
        sd = SB(base + 2048, [512], F32)
        t1 = SB(base + 4096, [512], F32)
        t2 = SB(base + 6144, [512], F32)
        kd3 = int(os.environ.get("KD3", "99"))
        ACT(sq[:], PS(b_in), AF.Square, [pst(b_in)], ["qsq"])
        TS(kg[:], PS(b_in), pv[:, gcol:gcol + 1], None, ALU.mult, ALU.bypass, [pst(b_in), "pv", "qsq"], ["qkg"])
        if kd3 >= 2:
            PE([(PS(6), bones, sq[:], True, True)], ["qsq", "cm"], [pst(6)])
            PE([(PS(7), pswap, kg[:], True, True)], ["qkg", "cm"], [pst(7)])
        if kd3 >= 3:
            ACT(sd[:], PS(6), AF.Sqrt, [pst(6)], ["qsd"], scale=1.0 / 64, bias=EPS)
            DVE("reciprocal", ["qsd"], ["qsd"], out=sd[:], in_=sd[:])
        if kd3 >= 4:
            TT(t1[:], kg[:], ctab, ALU.mult, ["qkg", "rope"], ["qt1"])
        if kd3 >= 5:
            TT(t2[:], PS(7), stab, ALU.mult, [pst(7), "rope"], ["qt2"])
            TT(t1[:], t1[:], t2[:], ALU.add, ["qt1", "qt2"], ["qt1"])
        if kd3 >= 6:
            TT(out, t1[:], sd[:], ALU.mult, ["qt1", "qsd"], [wtok])

    QKP = 8192

    def mix_kv(l):
        wk = DIN("wk%d" % l, [128, 8, 128])
        wv = DIN("wv%d" % l, [128, 8, 128])
        ropec = DIN("ropec", [128, T])
        ropes = DIN("ropes", [128, T])
        if not FUSED:
            kt_o = DOUT("kt_o%d" % l, [128, T])
            v_o = DOUT("v_o%d" % l, [128, NT, 128])
        b = G0
        wks = SB(b, [8, 128], BF16); b += 2048
        wvs = SB(b, [8, 128], BF16); b += 2048
        KT = SB(b, [T], F32); b += 8192
        Vs = SB(b, [NT, 192 if FUSED else 128], F32); b += 12288
        if FUSED:
            DVE("memset", [], ["Vs1"], ap=Vs[:, :, 64:128], constant=1.0)
        ct = SB(b, [T], F32); b += 8192
        stb = SB(b, [T], F32); b += 8192
        qb = b
        LW(wks[:], wk, ("wk", 0))
        LW(wvs[:], wv, ("wv", 0))
        DMA("sp", ct[:], ropec, [], ["rope"], "ropec")
        DMA("sp", stb[:], ropes, [], ["rope"], "ropes")
        kd2 = int(os.environ.get("KD2", "99"))
        for tb in range(4):
            tok = slice(tb * 512, tb * 512 + 512)
            bi = tb % 2
            if kd2 >= 2:
                PE([(PS(bi), wks[:, c, :], hT[:, c, tok], c == 0, c == 7) for c in range(8)], [("wk", 0), "hT"], [pst(bi)])
            if kd2 >= 3:
                qk_post(bi, 49, tb, qb, KT[:, tok], "KT", ct[:, tok], stb[:, tok])
        for t in range(NT):
            bi = 2 + t % 4
            if kd2 >= 4:
                PE([(PS(bi, 128), hT[:, c, t * 128:(t + 1) * 128], wvs[:, c, :], c == 0, c == 7) for c in range(8)], [("wv", 0), "hT"], [pst(bi)])
                if FUSED:
                    ACT(Vs[:, t, 0:64], PS(bi, 128)[:, 0:64], AF.Copy, [pst(bi)], ["Vs"])
                    ACT(Vs[:, t, 128:192], PS(bi, 128)[:, 64:128], AF.Copy, [pst(bi)], ["Vs"])
                else:
                    ACT(Vs[:, t, :], PS(bi, 128), AF.Copy, [pst(bi)], ["Vs"])
        if FUSED:
            DMA_DYN("sp", lambda e, par: (sh_kt[l][bass.ds(par, 1), :, :], KT[:]), ["KT"], ["kt_o"], "kt_o")
            DMA_DYN("sp", lambda e, par: (sh_v[l][bass.ds(par, 1), :, :, :], Vs[:]), ["Vs", "Vs1"], ["v_o"], "v_o")
        elif kd2 >= 5:
            DMA("sp", kt_o, KT[:], ["KT"], ["kt_o"], "kt_o")
            DMA("sp", v_o, Vs[:], ["Vs"], ["v_o"], "v_o")
        P.barrier()

    LR_HF = G0
    LR_XR = G0 + 16384
    LR_XFT = LR_XR + 16448
    LR_END = LR_XFT + 128

    def lru_gates(xc, n, d, c, A, a_out, bx_out, tag, ri, th):
        wbd = DIN("wbd%d" % cur_l[0], [128, 4, 4, 128])
        wb = A.t([2, 128], BF16)
        LW(wb[:], wbd[:, c, 2 * d:2 * d + 2, :], ("wbd", 0))
        nblk = (n + 511) // 512
        banks = [pst(q) for q in range(nblk)]
        ba = pv[:, 78 + 8 * d + c:79 + 8 * d + c]
        bi_ = pv[:, 82 + 8 * d + c:83 + 8 * d + c]
        nl8 = pvx[:, 4 * d + c:4 * d + c + 1]
        pnl8 = pvx[:, 8 + 4 * d + c:8 + 4 * d + c + 1]
        for which, dst, bias, wt in ((0, ri, ba, tag + "r"), (1, bx_out, bi_, tag + "bx")):
            mm = []
            for q in range(nblk):
                w_ = min(512, n - q * 512)
                mm.append((psum[:, q * 512:q * 512 + w_], wb[:, which, :], xc[:, q * 512:q * 512 + w_], True, True))
            PE(mm, [("wbd", 0), tag + "xc"], banks)
            ACT(dst, psum[:, 0:n], AF.Sigmoid, banks + ["pv"], [wt], bias=bias, scale=1.0)
        ACT(a_out, ri, AF.Exp, [tag + "r", "pvx"], [tag + "a"], scale=nl8)
        ACT(th, ri, AF.Tanh, [tag + "r", "pvx"], [tag + "th"], scale=pnl8)
        ACT(ri, a_out, AF.Square, [tag + "a"], [tag + "r"])
        STT(ri, ri, 1.0, th, ALU.add, ALU.mult, [tag + "r", tag + "th"], [tag + "r"])
        ACT(ri, ri, AF.Sqrt, [tag + "r"], [tag + "r"])
        TT(bx_out, bx_out, xc, ALU.mult, [tag + "bx", tag + "xc"], [tag + "bx"])
        TT(bx_out, bx_out, ri, ALU.mult, [tag + "bx", tag + "r"], [tag + "bx"])

    cur_l = [0]

    def conv(out_bf, xin, n, wcol, cbcol, acc, r, w, tag):
        TS(acc, xin[:, 0:n], pv[:, wcol:wcol + 1], pv[:, cbcol:cbcol + 1], ALU.mult, ALU.add, r + ["pv"], [tag + "acc"])
        for o in range(1, 5):
            dst = out_bf if o == 4 else acc
            STT(dst, xin[:, o:o + n], pv[:, wcol + o:wcol + o + 1], acc, ALU.mult, ALU.add,
                r + ["pv", tag + "acc"], [tag + "acc"] if o < 4 else w)

    def reverse_blocks(dst_fn, src_fn, nblk, T4, r, w, tag):
        for g in range(nblk // 4):
            bt = g % 2
            PE([(PS(bt)[:, q * 128:(q + 1) * 128], src_fn(4 * g + q), ident, True, True) for q in range(4)], r + ["cm"], [pst(bt)])
            ACT(T4[bt][:], PS(bt), AF.Copy, [pst(bt)], [(tag + "T4", bt)])
            PE([(PS(4 + bt)[:, q * 128:(q + 1) * 128], T4[bt][:, q * 128:(q + 1) * 128], Jm, True, True) for q in range(4)],
               [(tag + "T4", bt), "cm"], [pst(4 + bt)])
            for q in range(4):
                DVE("tensor_copy", [pst(4 + bt)], w, out=dst_fn(nblk - 1 - (4 * g + q)), in_=PS(4 + bt)[:, q * 128:(q + 1) * 128])

    def lru_pre(l, send):
        cur_l[0] = l
        wlx = DIN("wlx%d" % l, [128, 8, 512])
        hF = SB(LR_HF, [4, T], BF16)
        XR = SB(LR_XR, [4, 2056], BF16)
        XFt = SB(LR_XFT, [4, 16], BF16)
        A = Bump(LR_END)
        wl = A.t([8, 512], BF16)
        XF = A.t([2052], BF16)
        xc = A.t([2048], BF16)
        acc = A.t([2048], F32)
        aa = A.t([2048], F32)
        bx = A.t([2048], F32)
        ri = A.t([2048], F32)
        T4 = [A.t([512], BF16), A.t([512], BF16)]
        hs = A.t([4], F32)
        tlf = A.t([4, 4], F32)
        LW(wl[:], wlx, ("wlx", 0))
        DVE("memset", [], ["XRz"], ap=XR[:, :, 2052:2056], constant=0.0)
        for c in range(4):
            tg = "F"
            DVE("memset", [], [tg + "XF"], ap=XF[:, 0:2], constant=0.0)
            DVE("memset", [], [tg + "XF"], ap=XF[:, 2050:2052], constant=0.0)
            for q in range(4):
                PE([(PS(q), wl[:, cc, c * 128:(c + 1) * 128], hT[:, cc, q * 512:(q + 1) * 512], cc == 0, cc == 7) for cc in range(8)],
                   [("wlx", 0), "hT"], [pst(q)])
            kd5 = int(os.environ.get("KD5", "99"))
            ACT(XF[:, 2:2050], psum[:, 0:2048], AF.Copy, [pst(q) for q in range(4)], [tg + "XF"])
            DVE("tensor_copy", [tg + "XF"], ["XFt"], out=XFt[:, c, 0:8], in_=XF[:, 2042:2050])
            if kd5 >= 2:
                reverse_blocks(lambda bb, c=c: XR[:, c, 4 + bb * 128:4 + (bb + 1) * 128],
                               lambda bb: XF[:, 2 + bb * 128:2 + (bb + 1) * 128], 16, T4, [tg + "XF"], ["XR"], tg)
            if kd5 >= 3:
                conv(xc[:], XF[:], 2048, 102 + 5 * c, 74 + c, acc[:], [tg + "XF"], [tg + "xc"], tg)
            A2 = Bump(A.p)
            if kd5 >= 4:
                lru_gates(xc[:], 2048, 0, c, A2, aa[:], bx[:], tg, ri[:], acc[:])
            if kd5 >= 5:
                DVE("tensor_tensor_scan", [tg + "a", tg + "bx"], ["hF"], out=hF[:, c, :], data0=aa[:], data1=bx[:], initial=0.0,
                    op0=ALU.mult, op1=ALU.add)
            DVE("tensor_copy", ["hF"], ["hs"], out=hs[:, c:c + 1], in_=hF[:, c, 2045:2046])
        if send and FUSED:
            ftile = A.t([16], I32)
            DVE("tensor_copy", ["XFt"], ["tlf"], out=tlf[:], in_=XFt[:, :, 4:8])
            DMA_DYN("sp", lambda e, par: (sh_tl[l][bass.ds(par, 1), :, :], tlf[:].rearrange("p a b -> p (a b)")), ["tlf"], ["tl_o"], "tl_o")
            DMA_DYN("sp", lambda e, par: (sh_hs[l][bass.ds(par, 1), :, :], hs[:]), ["hs"], ["hs_o"], "hs_o")
            DVE("memset", [], ["ftile"], ap=ftile[0:1, :], constant=MAGIC[l])
            DMA_DYN("sp", lambda e, par: (sh_flag[l][bass.ds(par, 1), :], ftile[0:1, :]),
                    ["ftile", "tl_o", "hs_o", "kt_o", "v_o"], ["flag_o"], "flag_o")
        elif send:
            tl_o = DOUT("tl_o%d" % l, [128, 4, 4])
            hs_o = DOUT("hs_o%d" % l, [128, 4])
            DVE("tensor_copy", ["XFt"], ["tlf"], out=tlf[:], in_=XFt[:, :, 4:8])
            DMA("sp", tl_o, tlf[:], ["tlf"], ["tl_o"], "tl_o")
            DMA("sp", hs_o, hs[:], ["hs"], ["hs_o"], "hs_o")
            DMA("sp", DOUT("spo_hF", [128, 4, T], BF16), hF[:], ["hF"], ["spo_hF"], "spo_hF")
            DMA("sp", DOUT("spo_XR", [128, 4, 2056], BF16), XR[:], ["XR", "XRz"], ["spo_XR"], "spo_XR")
            DMA("sp", DOUT("spo_XFt", [128, 4, 16], BF16), XFt[:], ["XFt"], ["spo_XFt"], "spo_XFt")
            DMA("sp", DOUT("spo_hT", [128, 8, T], BF16), hT[:], ["hT"], ["spo_hT"], "spo_hT")
        P.barrier_keep(["kt_o", "v_o", "tl_o", "hs_o", "flag_o"]) if FUSED else P.barrier()

    def lru_post(l):
        cur_l[0] = l
        wly = DIN("wly%d" % l, [128, 8, 512])
        if not FUSED:
            tl_i = DIN("tl_i%d" % l, [128, 4, 4])
            hs_i = DIN("hs_i%d" % l, [128, 4])
        hF = SB(LR_HF, [4, T], BF16)
        XR = SB(LR_XR, [4, 2056], BF16)
        XFt = SB(LR_XFT, [4, 16], BF16)
        ylru = SB(YL_OFF, [4, T], BF16)
        A = Bump(LR_END, YL_OFF)
        wl = A.t([8, 128], BF16)
        xcr = A.t([2052], BF16)
        acc = A.t([2052], F32)
        aa = A.t([2052], F32)
        bx = A.t([2052], F32)
        ri = A.t([2052], F32)
        hR = A.t([2052], BF16)
        tls = A.t([4, 4], F32)
        hsb = A.t([4], F32)
        hsf = A.t([4], F32)
        h45 = A.t([4], F32)
        sc = A.t([64], F32)
        xct = A.t([16], BF16)
        T4 = [A.t([512], BF16), A.t([512], BF16)]
        wbs = A.t([2, 128], BF16)
        hB = xcr
        if FUSED:
            def spin(e, l=l):
                oth = 1 - get_par(e, "sp")

                def cond():
                    v = e.value_load(sh_flag[l][bass.ds(oth, 1), 0:1])
                    return v - MAGIC[l]
                with e.While(cond):
                    e.nop()
                return e.nop()
            P.add("sp", spin, r=["flag_o"], w=[("spun", l)])
            DMA_DYN("sp", lambda e, par: (tls[:].rearrange("p a b -> p (a b)"), sh_tl[l][bass.ds(1 - par, 1), :, :]), [("spun", l)], ["tls"], "tls")
            DMA_DYN("sp", lambda e, par: (hsb[:], sh_hs[l][bass.ds(1 - par, 1), :, :]), [("spun", l)], ["hsb"], "hsb")
        else:
            DMA("sp", tls[:], tl_i, [], ["tls"], "tls")
            DMA("sp", hsb[:], hs_i, [], ["hsb"], "hsb")
        DVE("tensor_copy", ["hsb"], ["hsf"], out=hsf[:], in_=hsb[:])
        for c in range(4):
            tg = "B"
            LW(wl[:], wly[:, :, c * 128:(c + 1) * 128], ("wly", 0))
            DVE("tensor_copy", ["tls", "XFt"], ["XFt"], out=XFt[:, c, 8:9], in_=tls[:, c, 3:4])
            DVE("tensor_copy", ["tls", "XFt"], ["XFt"], out=XFt[:, c, 9:10], in_=tls[:, c, 2:3])
            conv(xct[:, 0:4], XFt[:, c, 2:10], 4, 102 + 5 * c, 74 + c, sc[:, 0:4], ["XFt"], [tg + "txc"], tg + "t")
            lru_gates_small(xct[:, 0:4], 4, 0, c, sc, tg + "t", wbs)
            DVE("tensor_copy", ["hF"], ["h45"], out=h45[:, c:c + 1], in_=hF[:, c, 2045:2046])
            DVE("tensor_tensor_scan", [tg + "ta", tg + "tbx", "hF", "h45"], ["hF"], out=hF[:, c, 2046:2048], data0=sc[:, 18:20], data1=sc[:, 26:28],
                initial=h45[:, c:c + 1], op0=ALU.mult, op1=ALU.add)
            DVE("tensor_copy", ["tls", "XR"], ["XR"], out=XR[:, c, 0:4], in_=tls[:, c, :])
            conv(xcr[:, 0:2050], XR[:, c, 0:2054], 2050, 122 + 5 * c, 74 + c, acc[:, 0:2050], ["XR", "XRz"], [tg + "xc"], tg)
            A2 = Bump(A.p, YL_OFF)
            lru_gates(xcr[:, 0:2050], 2050, 1, c, A2, aa[:, 0:2050], bx[:, 0:2050], tg, ri[:, 0:2050], acc[:, 0:2050])
            DVE("tensor_tensor_scan", [tg + "a", tg + "bx", "hsf"], [tg + "hR"], out=hR[:, 0:2050], data0=aa[:, 0:2050], data1=bx[:, 0:2050],
                initial=hsf[:, c:c + 1], op0=ALU.mult, op1=ALU.add)
            reverse_blocks(lambda bb: hB[:, bb * 128:(bb + 1) * 128],
                           lambda bb: hR[:, 2 + bb * 128:2 + (bb + 1) * 128], 16, T4, [tg + "hR"], [tg + "xc"], tg)
            for q in range(4):
                PE([(PS(q), wl[:, cc, :], hT[:, cc, q * 512:(q + 1) * 512], cc == 0, cc == 7) for cc in range(8)],
                   [("wly", 0), "hT"], [pst(q)])
            gelu(aa[:, 0:2048], psum[:, 0:2048], 2048, acc[:, 0:2048], bx[:, 0:2048], [pst(q) for q in range(4)], [tg + "gl"], tg + "g")
            TT(hB[:, 0:2048], hB[:, 0:2048], hF[:, c, :], ALU.add, [tg + "xc", "hF"], [tg + "xc"])
            TT(ylru[:, c, :], hB[:, 0:2048], aa[:, 0:2048], ALU.mult, [tg + "xc", tg + "gl"], ["ylru"])
        P.barrier_keep([("spun", l)]) if FUSED else P.barrier()

    def lru_gates_small(xc, n, d, c, sc, tag, wb):
        wbd = DIN("wbd%d" % cur_l[0], [128, 4, 4, 128])
        LW(wb[:], wbd[:, c, 2 * d:2 * d + 2, :], ("wbds", 0))
        PE([(psum[:, 0:n], wb[:, 0, :], xc, True, True)], [("wbds", 0), tag[:-1] + "txc"], [pst(0)])
        PE([(psum[:, 512:512 + n], wb[:, 1, :], xc, True, True)], [("wbds", 0), tag[:-1] + "txc"], [pst(1)])
        ba = pv[:, 78 + 8 * d + c:79 + 8 * d + c]
        bi_ = pv[:, 82 + 8 * d + c:83 + 8 * d + c]
        nl8 = pvx[:, 4 * d + c:4 * d + c + 1]
        pnl8 = pvx[:, 8 + 4 * d + c:8 + 4 * d + c + 1]
        r_ = sc[:, 8:8 + n]
        a_ = sc[:, 16:16 + n]
        bx_ = sc[:, 24:24 + n]
        th_ = sc[:, 32:32 + n]
        ACT(r_, psum[:, 0:n], AF.Sigmoid, [pst(0), "pv"], [tag + "r"], bias=ba, scale=1.0)
        ACT(bx_, psum[:, 512:512 + n], AF.Sigmoid, [pst(1), "pv"], [tag + "bx"], bias=bi_, scale=1.0)
        ACT(a_, r_, AF.Exp, [tag + "r", "pvx"], [tag + "a"], scale=nl8)
        ACT(th_, r_, AF.Tanh, [tag + "r", "pvx"], [tag + "th"], scale=pnl8)
        ACT(r_, a_, AF.Square, [tag + "a"], [tag + "r"])
        STT(r_, r_, 1.0, th_, ALU.add, ALU.mult, [tag + "r", tag + "th"], [tag + "r"])
        ACT(r_, r_, AF.Sqrt, [tag + "r"], [tag + "r"])
        TT(bx_, bx_, xc, ALU.mult, [tag + "bx", tag[:-1] + "txc"], [tag + "bx"])
        TT(bx_, bx_, r_, ALU.mult, [tag + "bx", tag + "r"], [tag + "bx"])

    def attention(l):
        wq = DIN("wq%d" % l, [128, 8, 512])
        if not FUSED:
            kt_i = DIN("kt_i%d" % l, [128, 2 * T])
            va_i = DIN("va_i%d" % l, [128, 32, 2, 128])
        ropec = DIN("ropec", [128, T])
        ropes = DIN("ropes", [128, T])
        AO = SB(AO_OFF, [4, T], BF16)
        b = G0
        KT = SB(b, [2 * T], BF16); b += 8192
        VA = SB(b, [32, 192] if FUSED else [32, 2, 128], BF16); b += 16384
        wqs = SB(b, [8, 512], BF16); b += 8192
        ct = SB(b, [512], F32); b += 2048
        stb = SB(b, [512], F32); b += 2048
        Qj = [SB(b, [512], BF16), SB(b + 1024, [512], BF16)]; b += 2048
        PT = [SB(b + i * 1024, [512], BF16) for i in range(4)]; b += 4096
        rc = SB(b, [512], F32); b += 2048
        qb = b
        assert qb + QKP <= AO_OFF, qb + QKP
        LW(wqs[:], wq, ("wq", 0))
        if FUSED:
            DMA("pool", KT[:].rearrange("p (r t) -> p r t", r=2), sh_kt[l].rearrange("r p t -> p r t"), [("spun", l)], ["KTa"], "KTa")
            for r_ in range(2):
                DMA("pool", VA[:, r_ * 16:(r_ + 1) * 16, :], sh_v[l][r_, :, :, :], [("spun", l)], [("VAh", r_)], "VA%d0" % r_)
        else:
            DMA("pool", KT[:], kt_i, [], ["KTa"], "KTa")
            DMA("pool", VA[:], va_i, [], ["VA"], "VA")
        it = 0
        chunks = [(tb, j) for tb in range(4) for j in range(4)]

        def prologue(ci):
            tb, j = chunks[ci]
            tok = slice(tb * 512, tb * 512 + 512)
            if j == 0:
                DMA("sp", ct[:], ropec[:, tok], [], ["rope"], "ropec")
                DMA("sp", stb[:], ropes[:, tok], [], ["rope"], "ropes")
            qj = Qj[ci % 2]
            qtok = ("Qj", ci % 2)
            PE([(PS(5), wqs[:, c, j * 128:(j + 1) * 128], hT[:, c, tok], c == 0, c == 7) for c in range(8)], [("wq", 0), "hT"], [pst(5)])
            qk_post(5, 48, tb, qb, qj[:], qtok, ct[:], stb[:])

        prologue(0)
        for ci, (tb, j) in enumerate(chunks):
            tok = slice(tb * 512, tb * 512 + 512)
            qj = Qj[ci % 2]
            qtok = ("Qj", ci % 2)
            for kv in range(2):
                r0 = kv * 64
                ob = 3 + (it % 2)
                it += 1

                def qk(kt, kv=kv, r0=r0, qj=qj, qtok=qtok):
                    sb_ = kt % 3
                    PE([(PS(sb_), KT[r0:r0 + 64, kt * 128:(kt + 1) * 128], qj[r0:r0 + 64, :], True, True)], ["KTa", qtok], [pst(sb_)])

                def ex(kt):
                    sb_ = kt % 3
                    ACT(PT[kt % 4][:], PS(sb_), AF.Exp, [pst(sb_)], [("PT", kt % 4)], scale=0.125)

                def pvm(kt, kv=kv, ob=ob):
                    va_l = VA[:, kt, kv * 64:kv * 64 + 128] if FUSED else VA[:, kt, kv, :]
                    PE([(PS(ob), va_l, PT[kt % 4][:], kt == 0, kt == 31)], [("VAh", kt // 16) if FUSED else "VA", ("PT", kt % 4)], [pst(ob)])
                qk(0)
                qk(1)
                for kt in range(32):
                    ex(kt)
                    if kt + 2 < 32:
                        qk(kt + 2)
                    pvm(kt)
                    if kv == 1 and kt == 6 and ci + 1 < len(chunks):
                        prologue(ci + 1)
                o0 = 64 - r0
                DVE("reciprocal", [pst(ob)], ["rc"], out=rc[r0:r0 + 64, :], in_=PS(ob)[o0:o0 + 64, :])
                TT(AO[r0:r0 + 64, j, tok], PS(ob)[r0:r0 + 64, :], rc[r0:r0 + 64, :], ALU.mult, [pst(ob), "rc"], ["AO"])
        P.barrier()

    def gmlp(l):
        wgu = DIN("wgu%d" % l, [128, 8, 512])
        wgv = DIN("wgv%d" % l, [128, 8, 512])
        wst = DIN("wst%d" % l, [128, 4, 128])
        yg = SB(YG_OFF, [4, T], BF16)
        b = G0
        wu = SB(b, [8, 512], BF16); b += 8192
        wv_ = SB(b, [8, 512], BF16); b += 8192
        ws_ = SB(b, [4, 128], BF16); b += 1024
        gv = [SB(b, [512], F32), SB(b + 2048, [512], F32)]; b += 4096
        s0 = [SB(b + i * 2048, [512], F32) for i in range(3)]; b += 3 * 2048
        s1 = [SB(b + i * 2048, [512], F32) for i in range(3)]; b += 3 * 2048
        vn = [SB(b, [512], BF16), SB(b + 1024, [512], BF16)]; b += 2048
        gu = SB(b, [4, 512], BF16); b += 4096
        sv = [SB(b, [512], F32), SB(b + 2048, [512], F32)]; b += 4096
        junk = SB(b, [512], BF16); b += 1024
        assert b <= YG_OFF
        LW(wu[:], wgu, ("wgu", 0))
        LW(wv_[:], wgv, ("wgv", 0))
        LW(ws_[:], wst, ("wst", 0))
        for tb in range(4):
            tok = slice(tb * 512, tb * 512 + 512)
            for g in range(4):
                PE([(PS(g % 2), wu[:, c, g * 128:(g + 1) * 128], hT[:, c, tok], c == 0, c == 7) for c in range(8)], [("wgu", 0), "hT"], [pst(g % 2)])
                gelu(gu[:, g, :], PS(g % 2), 512, s0[2][:], s1[2][:], [pst(g % 2)], [("gu", g)], "gu")
            for tt in range(4):
                t = tb * 4 + tt
                k = t % 2
                PE([(PS(2 + k), hT[:, c, t * 128:(t + 1) * 128], wv_[:, c, :], c == 0, c == 7) for c in range(8)], [("wgv", 0), "hT"], [pst(2 + k)])
                gelu(gv[k][:], PS(2 + k), 512, s0[k][:], s1[k][:], [pst(2 + k)], [("gv", k)], "gv%d" % k)
                ss = sm[:, 16 + k * 4:17 + k * 4]
                rs = sm[:, 17 + k * 4:18 + k * 4]
                stk = ("gst", k)
                P.add("act", lambda e, k=k, ss=ss: e.activation(out=junk[:], in_=gv[k][:], func=AF.Square, accum_out=ss),
                      r=[("gv", k)], w=["gjunk", stk])
                ACT(rs, ss, AF.Sqrt, [stk], [stk], scale=1.0 / 512, bias=EPS)
                DVE("reciprocal", [stk], [stk], out=rs, in_=rs)
                STT(vn[k][:], gv[k][:], rs, pv[:, 142:654], ALU.mult, ALU.mult, [("gv", k), stk, "pv"], [("vn", k)])
                PE([(PS(4 + k)[:, g * 128:(g + 1) * 128], vn[k][:, g * 128:(g + 1) * 128], ws_[:, g, :], True, True) for g in range(4)],
                   [("vn", k), ("wst", 0)], [pst(4 + k)])
                TT(sv[k][:], PS(4 + k), pv[:, 654:1166], ALU.add, [pst(4 + k), "pv"], [("sv", k)])
                o = yg[:, :, t * 128:(t + 1) * 128]
                TT(o, gu[:, :, tt * 128:(tt + 1) * 128], sv[k][:].rearrange("p (g n) -> p g n", g=4), ALU.mult,
                   [("sv", k)] + [("gu", g) for g in range(4)], ["yg"])
        P.barrier()

    def merge(l):
        wgate = DIN("wgate%d" % l, [128, 24, 8, 128])
        wup = DIN("wup%d" % l, [128, 3, 8, 4, 128])
        wmo = DIN("wmo%d" % l, [128, 8, D])
        AO = SB(AO_OFF, [4, T], BF16)
        yl = SB(YL_OFF, [4, T], BF16)
        yg = SB(YG_OFF, [4, T], BF16)
        Y = [AO, yg, yl]
        ytok = ["AO", "yg", "ylru"]
        b = G0
        MG = SB(b, [8, T], BF16); b += 32768
        ngs = wslots("wg", b, [8, 128], 3); b += 3 * 2048
        nus = wslots("wu", b, [4, 128], 3); b += 3 * 1024
        gt = [SB(b, [512], F32), SB(b + 2048, [512], F32)]; b += 4096
        tm = [SB(b, [512], F32), SB(b + 2048, [512], F32)]; b += 4096
        assert b <= YG_OFF
        k = 0
        for f in range(8):
            ws = []
            for br in range(3):
                wg_, gtok = ngs()
                LW(wg_[:], wgate[:, br * 8 + f, :, :], gtok)
                wu_, utok = nus()
                LW(wu_[:], wup[:, br, f, :, :], utok)
                ws.append((wg_, gtok, wu_, utok))
            for tb in range(4):
                tok = slice(tb * 512, tb * 512 + 512)
                for br in range(3):
                    wg_, gtok, wu_, utok = ws[br]
                    bg, bu = (k % 4) * 2, (k % 4) * 2 + 1
                    PE([(PS(bg), wg_[:, c, :], hT[:, c, tok], c == 0, c == 7) for c in range(8)], [gtok, "hT"], [pst(bg)])
                    PE([(PS(bu), wu_[:, c, :], Y[br][:, c, tok], c == 0, c == 3) for c in range(4)], [utok, ytok[br]], [pst(bu)])
                    g_ = gt[k % 2]
                    bcol = 50 + br * 8 + f
                    ACT(g_[:], PS(bg), AF.Sigmoid, [pst(bg), "pv"], [("gt", k % 2)], bias=pv[:, bcol:bcol + 1], scale=1.0)
                    acc = tm[(k // 3) % 2]
                    atok = ("tm", (k // 3) % 2)
                    if br == 0:
                        TT(acc[:], g_[:], PS(bu), ALU.mult, [("gt", k % 2), pst(bu)], [atok])
                    else:
                        TT(g_[:], g_[:], PS(bu), ALU.mult, [("gt", k % 2), pst(bu)], [("gt", k % 2)])
                        dst = MG[:, f, tok] if br == 2 else acc[:]
                        TT(dst, acc[:], g_[:], ALU.add, [atok, ("gt", k % 2)], [("MG", f)] if br == 2 else [atok])
                    k += 1
        P.barrier()
        wm = SB(G0 + 32768, [8, D], BF16)
        LW(wm[:], wmo, ("wmo", 0))
        for t in range(NT):
            tl = slice(t * 128, (t + 1) * 128)
            for h in range(2):
                bb = (2 * t + h) % 8
                PE([(PS(bb), MG[:, f, tl], wm[:, f, h * 512:(h + 1) * 512], f == 0, f == 7) for f in range(8)],
                   [("wmo", 0)] + [("MG", f) for f in range(8)], [pst(bb)])
                xs = x_sb[:, t, h * 512:(h + 1) * 512]
                TT(xs, PS(bb), xs, ALU.add, [pst(bb), "x"], ["x"])
        P.barrier()

    def xattn(l):
        mem = DIN("mem", [128, 2, D])
        wq = DIN("xwq%d" % l, [128, 8, D])
        wkv = DIN("xwkv%d" % l, [128, 8, 2 * D])
        wo = DIN("xwo%d" % l, [128, 8, D])
        b = G0
        mems = SB(b, [2, D], F32); b += 8192
        memT = SB(b, [8, 256], BF16); b += 4096
        nb = b; b += 6144
        DMA("sp", mems[:], mem, [], ["mems"], "mems")
        norm_to_hT(24, src=mems, ntiles=2, dst=memT, base=nb, rtok="mems", wtok="memT")
        P.barrier()
        b = nb
        wkvs = SB(b, [8, 2 * D], BF16); b += 32768
        KM = SB(b, [8, 256], BF16); b += 4096
        VM = SB(b, [2, D], BF16); b += 4096
        LW(wkvs[:], wkv, ("xwkv", 0))
        for f in range(8):
            PE([(PS(f % 2, 256), wkvs[:, c, f * 128:(f + 1) * 128], memT[:, c, :], c == 0, c == 7) for c in range(8)], [("xwkv", 0), "memT"], [pst(f % 2)])
            ACT(KM[:, f, :], PS(f % 2, 256), AF.Copy, [pst(f % 2)], ["KM"])
        for m in range(2):
            for h in range(2):
                bb = 2 + (2 * m + h) % 2
                PE([(PS(bb), memT[:, c, m * 128:(m + 1) * 128], wkvs[:, c, D + h * 512:D + (h + 1) * 512], c == 0, c == 7) for c in range(8)],
                   [("xwkv", 0), "memT"], [pst(bb)])
                ACT(VM[:, m, h * 512:(h + 1) * 512], PS(bb), AF.Copy, [pst(bb)], ["VM"])
        P.barrier()
        wqs = SB(nb, [8, D], BF16)
        wos = SB(nb + 16384, [8, D], BF16)
        b = nb + 32768 + 8192
        qT = SB(b, [8, 512], BF16); b += 8192
        oT = SB(b, [8, 512], BF16); b += 8192
        PT = [SB(b, [512], BF16), SB(b + 1024, [512], BF16), SB(b + 2048, [512], BF16), SB(b + 3072, [512], BF16)]; b += 4096
        rc = SB(b, [512], F32); b += 2048
        ones = SB(b, [128], BF16); b += 256
        LW(wqs[:], wq, ("xwq", 0))
        LW(wos[:], wo, ("xwo", 0))
        DVE("memset", [], ["ones"], ap=ones[:], constant=1.0)
        pk = 0
        for tb in range(4):
            tok = slice(tb * 512, tb * 512 + 512)
            for f in range(8):
                PE([(PS(f % 2), wqs[:, c, f * 128:(f + 1) * 128], hT[:, c, tok], c == 0, c == 7) for c in range(8)], [("xwq", 0), "hT"], [pst(f % 2)])
                ACT(qT[:, f, :], PS(f % 2), AF.Copy, [pst(f % 2)], [("qT", f)])
            for hh in range(4):
                pts = []
                for m in range(2):
                    bb = 2 + m
                    PE([(PS(bb), KM[:, 2 * hh + e, m * 128:(m + 1) * 128], qT[:, 2 * hh + e, :], e == 0, e == 1) for e in range(2)],
                       ["KM", ("qT", 2 * hh), ("qT", 2 * hh + 1)], [pst(bb)])
                    p_ = PT[pk % 4]
                    ptk = ("xPT", pk % 4)
                    pk += 1
                    ACT(p_[:], PS(bb), AF.Exp, [pst(bb)], [ptk], scale=1.0 / 16)
                    pts.append((p_, ptk))
                PE([(PS(4), ones[:], pts[m][0][:], m == 0, m == 1) for m in range(2)], ["ones", pts[0][1], pts[1][1]], [pst(4)])
                DVE("reciprocal", [pst(4)], ["xrc"], out=rc[:], in_=PS(4))
                for e in range(2):
                    bb = 5 + e
                    PE([(PS(bb), VM[:, m, (2 * hh + e) * 128:(2 * hh + e + 1) * 128], pts[m][0][:], m == 0, m == 1) for m in range(2)],
                       ["VM", pts[0][1], pts[1][1]], [pst(bb)])
                    TT(oT[:, 2 * hh + e, :], PS(bb), rc[:], ALU.mult, [pst(bb), "xrc"], [("oT", 2 * hh + e)])
            for tt in range(4):
                t = tb * 4 + tt
                tl = slice(tt * 128, (tt + 1) * 128)
                for h in range(2):
                    bb = 6 + h
                    PE([(PS(bb), oT[:, f, tl], wos[:, f, h * 512:(h + 1) * 512], f == 0, f == 7) for f in range(8)],
                       [("xwo", 0)] + [("oT", f) for f in range(8)], [pst(bb)])
                    xs = x_sb[:, t, h * 512:(h + 1) * 512]
                    TT(xs, PS(bb), xs, ALU.add, [pst(bb), "x"], ["x"])
        P.barrier()

    def load_x(name):
        xin = DIN(name, [128, NT, D])
        for q in range(4):
            DMA("sp", x_sb[:, q * 4:(q + 1) * 4, :], xin[:, q * 4:(q + 1) * 4, :], [], ["x"], "xin%d" % q)

    def store_x(name):
        xo = DOUT(name, [128, NT, D])
        for q in range(4):
            DMA("sp", xo[:, q * 4:(q + 1) * 4, :], x_sb[:, q * 4:(q + 1) * 4, :], ["x"], ["xo%d" % q], "xo%d" % q)
        P.add("sp", None, r=["xo%d" % q for q in range(4)])

    def normp(gcol):
        norm_to_hT(gcol)
        P.barrier()

    dbg = int(os.environ.get("KDBG", "99"))

    def front(l):
        if dbg >= 1:
            normp(0)
        if dbg >= 2:
            ffn(l, 1)
        if dbg >= 3:
            normp(8)
        if dbg >= 4:
            mix_kv(l)
        if dbg >= 5:
            lru_pre(l, True)

    kb = int(os.environ.get("KB", "99"))
    kdump = os.environ.get("KDUMP", "")

    def dump(off, tok):
        src = SB(off, [4, T], BF16)
        stg = [SB(G0, [T], F32), SB(G0 + 8192, [T], F32)]
        do = DOUT("dbg_o", [128, 4, T])
        for c in range(4):
            DVE("tensor_copy", [tok], [("stg", c % 2)], out=stg[c % 2][:], in_=src[:, c, :])
            DMA("sp", do[:, c, :], stg[c % 2][:], [("stg", c % 2)], [("dbgo", c)], "dbgo%d" % (c % 2))
        P.barrier()

    def back(l, recompute):
        if recompute:
            hF = SB(LR_HF, [4, T], BF16)
            XR = SB(LR_XR, [4, 2056], BF16)
            XFt = SB(LR_XFT, [4, 16], BF16)
            DMA("sp", hT[:], DIN("spi_hT", [128, 8, T], BF16), [], ["hT"], "spi_hT")
            DMA("sp", hF[:], DIN("spi_hF", [128, 4, T], BF16), [], ["hF"], "spi_hF")
            DMA("sp", XR[:], DIN("spi_XR", [128, 4, 2056], BF16), [], ["XR"], "spi_XR")
            DMA("sp", XFt[:], DIN("spi_XFt", [128, 4, 16], BF16), [], ["XFt"], "spi_XFt")
            P.barrier()
        if kb >= 1:
            lru_post(l)
        if kb >= 2:
            attention(l)
        if kb >= 3:
            gmlp(l)
        if kdump == "ylru":
            dump(YL_OFF, "ylru")
        if kdump == "AO":
            dump(AO_OFF, "AO")
        if kdump == "yg":
            dump(YG_OFF, "yg")
        if kb >= 4:
            merge(l)
        if kb >= 5:
            normp(16)
            xattn(l)
        if kb >= 6:
            normp(32)
            ffn(l, 2)

    def final_norm():
        gfin = DIN("gfin", [128, D])
        y = DOUT("y", [128, NT, D])
        gf = SB(G0, [D], F32)
        junk = SB(G0 + 4096, [D], BF16)
        yo = [SB(G0 + 8192, [D], F32), SB(G0 + 12288, [D], F32)]
        DMA("sp", gf[:], gfin, [], ["gf"], "gf")
        for t in range(NT):
            ss = sm[:, (t % 2) * 4:(t % 2) * 4 + 1]
            rs = sm[:, (t % 2) * 4 + 1:(t % 2) * 4 + 2]
            stk = ("fst", t % 2)
            P.add("act", lambda e, t=t, ss=ss: e.activation(out=junk[:], in_=x_sb[:, t, :], func=AF.Square, accum_out=ss),
                  r=["x"], w=["fjunk", stk])
            ACT(rs, ss, AF.Sqrt, [stk], [stk], scale=1.0 / D, bias=EPS)
            DVE("reciprocal", [stk], [stk], out=rs, in_=rs)
            STT(yo[t % 2][:], x_sb[:, t, :], rs, gf[:], ALU.mult, ALU.mult, ["x", stk, "gf"], [("yo", t % 2)])
            DMA("sp", y[:, t, :], yo[t % 2][:], [("yo", t % 2)], [("y", t)], "y%d" % (t % 2))
        P.add("sp", None, r=[("y", t) for t in range(NT)])
        P.barrier()

    if FUSED:
        zt = SB(G0, [16], I32)
        DVE("memset", [], ["zt"], ap=zt[0:1, :], constant=0)
        for l_ in range(2):
            DMA_DYN("sp", lambda e, par, l_=l_: (sh_flag[l_][bass.ds(par, 1), :], zt[0:1, :]), ["zt"], [("fz", l_)], "fz%d" % l_)
        load_x("x_in")
        load_pv(0)
        P.barrier()
        front(0)
        back(0, False)
        P.barrier()
        load_pv(1)
        P.barrier()
        front(1)
        back(1, False)
        final_norm()
        zt2 = SB(G0 + 65536, [16], I32)
        DVE("memset", [], ["zt2"], ap=zt2[0:1, :], constant=0)
        for l_ in range(2):
            DMA_DYN("sp", lambda e, par, l_=l_: (sh_flag[l_][bass.ds(par, 1), :], zt2[0:1, :]), ["zt2"], [("fz2", l_)], "fz%d" % l_)
    elif seg == 0:
        load_x("x_in")
        load_pv(0)
        front(0)
        store_x("x_out")
    elif seg == 1:
        load_x("x_in")
        load_pv(0)
        back(0, True)
        P.barrier()
        if kb >= 7:
            load_pv(1)
            front(1)
        store_x("x_out")
    else:
        load_x("x_in")
        load_pv(1)
        back(1, True)
        final_norm()
    P.barrier()
    P.emit()
    return nc, list(ins.keys()), list(outs.keys())


def _pcf(w):
    K, N = w.shape
    return np.ascontiguousarray(w.reshape(K // 128, 128, N).transpose(1, 0, 2))


def _colT(v, n):
    return np.ascontiguousarray(v.reshape(n, 128).T)


def _rope_tables(pos):
    n_freq = 16
    inv = (10000.0 ** (-np.arange(n_freq, dtype=np.float32) / n_freq)).astype(np.float32)
    row = (pos // 64).astype(np.float32)
    col = (pos % 64).astype(np.float32)
    ang_r = row[None, :] * inv[:, None]
    ang_c = col[None, :] * inv[:, None]
    C = np.zeros((64, len(pos)), np.float32)
    S = np.zeros((64, len(pos)), np.float32)
    for half, ang in ((0, ang_r), (1, ang_c)):
        o = half * 32
        C[o:o + 16] = np.cos(ang)
        C[o + 16:o + 32] = np.cos(ang)
        S[o:o + 16] = -np.sin(ang)
        S[o + 16:o + 32] = np.sin(ang)
    return np.concatenate([C, C], 0), np.concatenate([S, S], 0)


def _consts():
    cm = np.zeros((128, 4, 128), np.float32)
    cm[:, 0, :] = np.eye(128)
    cm[:, 1, :] = np.eye(128)[::-1]
    for h in range(2):
        cm[h * 64:(h + 1) * 64, 2, h * 64:(h + 1) * 64] = 1.0
    for p in range(128):
        sp = p + 16 if (p % 32) < 16 else p - 16
        cm[sp, 3, p] = 1.0
    return cm


def _layer_inputs(I, l, odd):
    d = {}
    f32 = np.float32
    for which, nm in ((1, "ffn1"), (2, "ffn2")):
        w_in = I[nm + "_w_in"][l]
        a = w_in[:, :DFF].reshape(8, 128, NJ, 128)
        b = w_in[:, DFF:].reshape(8, 128, NJ, 128)
        ab = np.concatenate([a, b], axis=3)
        d["f%d_win%d" % (which, l)] = np.ascontiguousarray(ab.transpose(1, 2, 0, 3))
        d["f%d_wout%d" % (which, l)] = _pcf(I[nm + "_w_out"][l])
    wmi = I["w_mix_in"][l]
    q = wmi[:, 0:512].reshape(D, 2, 4, 64).transpose(0, 2, 1, 3).reshape(D, 512)
    d["wq%d" % l] = _pcf(q)
    d["wk%d" % l] = _pcf(wmi[:, 512:640])
    d["wv%d" % l] = _pcf(wmi[:, 640:768])
    d["wgu%d" % l] = _pcf(wmi[:, 768:1280])
    d["wgv%d" % l] = _pcf(wmi[:, 1280:1792])
    d["wlx%d" % l] = _pcf(wmi[:, 1792:2304])
    d["wly%d" % l] = _pcf(wmi[:, 2304:2816])
    g = wmi[:, 2816:].reshape(8, 128, 24, 128)
    d["wgate%d" % l] = np.ascontiguousarray(g.transpose(1, 2, 0, 3))
    au = I["attn_up"][l].reshape(2, 4, 64, D).transpose(1, 0, 2, 3).reshape(512, D)
    ups = np.stack([au, I["gmlp_up"][l], I["lru_up"][l]], 0)
    ups = ups.reshape(3, 4, 128, 8, 128)
    d["wup%d" % l] = np.ascontiguousarray(ups.transpose(2, 0, 3, 1, 4))
    d["wmo%d" % l] = _pcf(I["w_mix_out"][l])
    d["xwq%d" % l] = _pcf(I["xattn_wq"][l])
    d["xwkv%d" % l] = _pcf(I["xattn_wkv"][l])
    d["xwo%d" % l] = _pcf(I["xattn_wo"][l])
    df, db = (1, 0) if odd else (0, 1)
    wbd = np.zeros((128, 4, 4, 128), f32)
    for c in range(4):
        for k, (nm, dd) in enumerate((("lru_wa", df), ("lru_wi", df), ("lru_wa", db), ("lru_wi", db))):
            for h in range(2):
                wbd[h * 64:(h + 1) * 64, c, k, h * 64:(h + 1) * 64] = I[nm][l, dd, 2 * c + h]
    d["wbd%d" % l] = wbd
    ws = I["gmlp_ws"][l]
    bs = I["gmlp_bs"][l]
    if odd:
        ws = ws[:, ::-1, ::-1]
        bs = bs[:, ::-1]
    d["wst%d" % l] = np.ascontiguousarray(ws.transpose(2, 0, 1))
    pv = np.zeros((128, NPV), f32)
    pv[:, 0:8] = _colT(I["ffn1_norm"][l], 8)
    pv[:, 8:16] = _colT(I["mix_norm"][l], 8)
    pv[:, 16:24] = _colT(I["xattn_norm"][l], 8)
    pv[:, 24:32] = _colT(I["mem_norm"][l], 8)
    pv[:, 32:40] = _colT(I["ffn2_norm"][l], 8)
    pv[:, 48] = np.tile(I["q_norm"][l], 2)
    pv[:, 49] = np.tile(I["k_norm"][l], 2)
    pv[:, 50:74] = _colT(I["b_gate"][l], 24)
    pv[:, 74:78] = _colT(I["lru_conv_b"][l], 4)
    pv[:, 78:82] = _colT(I["lru_ba"][l, df], 4)
    pv[:, 82:86] = _colT(I["lru_bi"][l, df], 4)
    pv[:, 86:90] = _colT(I["lru_ba"][l, db], 4)
    pv[:, 90:94] = _colT(I["lru_bi"][l, db], 4)
    pv[:, 94:98] = _colT(I["lru_lambda"][l, df], 4)
    pv[:, 98:102] = _colT(I["lru_lambda"][l, db], 4)
    cw = I["lru_conv_w"][l]
    w5 = np.zeros((5, 512), f32)
    w5[0:4] = cw
    if odd:
        w5 = w5[::-1]
    for c in range(4):
        pv[:, 102 + 5 * c:107 + 5 * c] = w5[:, c * 128:(c + 1) * 128].T
        pv[:, 122 + 5 * c:127 + 5 * c] = w5[::-1, c * 128:(c + 1) * 128].T
    pv[:, 142:654] = I["gmlp_v_norm"][l][None, :]
    pv[:, 654:1166] = bs.reshape(1, 512)
    d["pv%d" % l] = pv
    return d


_CACHE = {}


def _get_prog(seg):
    if seg not in _CACHE:
        _CACHE[seg] = build(seg)
    return _CACHE[seg]


def _run(seg, per_core):
    nc, in_names, out_names = _get_prog(seg)
    in_maps = [{k: np.ascontiguousarray(pc[k]) for k in in_names} for pc in per_core]
    res = run_bass_kernel_spmd(nc, in_maps, core_ids=list(range(8)))
    return res.results


def prepare(I):
    I = {k: np.asarray(v) for k, v in I.items()}
    x = I["x"]
    cm = _consts()
    per_core = []
    for c in range(8):
        b, odd = c // 2, c % 2
        xs = x[b, T:][::-1] if odd else x[b, :T]
        pos = (np.arange(T, 2 * T)[::-1] if odd else np.arange(T)).astype(np.int64)
        C_, S_ = _rope_tables(pos)
        d = {"x_in": np.ascontiguousarray(xs.reshape(NT, 128, D).transpose(1, 0, 2)),
             "cm": cm, "ropec": C_, "ropes": S_,
             "mem": np.ascontiguousarray(I["mem"][b].reshape(2, 128, D).transpose(1, 0, 2)),
             "gfin": np.ascontiguousarray(np.broadcast_to(I["final_norm"][None, :], (128, D)))}
        for l in range(2):
            d.update(_layer_inputs(I, l, odd))
        per_core.append(d)
    return per_core


def exchange(per_core, res, l):
    for c in range(8):
        p = c ^ 1
        e, o = c - c % 2, c - c % 2 + 1
        kt = np.concatenate([np.asarray(res[e]["kt_o%d" % l]), np.asarray(res[o]["kt_o%d" % l])], axis=1)
        v = np.concatenate([np.asarray(res[e]["v_o%d" % l]), np.asarray(res[o]["v_o%d" % l])], axis=1)
        va = np.ones((128, 32, 2, 128), dtype=v.dtype)
        va[:, :, 0, 0:64] = v[:, :, 0:64]
        va[:, :, 1, 64:128] = v[:, :, 64:128]
        per_core[c]["kt_i%d" % l] = kt
        per_core[c]["va_i%d" % l] = va
        per_core[c]["tl_i%d" % l] = np.asarray(res[p]["tl_o%d" % l])
        per_core[c]["hs_i%d" % l] = np.asarray(res[p]["hs_o%d" % l])
        per_core[c]["x_in"] = np.asarray(res[c]["x_out"])
        for nm in ("hT", "hF", "XR", "XFt"):
            per_core[c]["spi_" + nm] = np.asarray(res[c]["spo_" + nm])


def kernel(**I):
    per_core = prepare(I)
    B = np.asarray(I["x"]).shape[0]
    if os.environ.get("KFUSED", "1") == "1":
        r2 = _run("fused", per_core)
        out = np.zeros((B, 2 * T, D), np.float32)
        for c in range(8):
            b, odd = c // 2, c % 2
            y = np.asarray(r2[c]["y"]).transpose(1, 0, 2).reshape(T, D)
            if odd:
                out[b, T:] = y[::-1]
            else:
                out[b, :T] = y
        return out
    r0 = _run(0, per_core)
    exchange(per_core, r0, 0)
    r1 = _run(1, per_core)
    exchange(per_core, r1, 1)
    r2 = _run(2, per_core)
    out = np.zeros((B, 2 * T, D), np.float32)
    for c in range(8):
        b, odd = c // 2, c % 2
        y = np.asarray(r2[c]["y"]).transpose(1, 0, 2).reshape(T, D)
        if odd:
            out[b, T:] = y[::-1]
        else:
            out[b, :T] = y
    return out
```
